# Optimizing a Trainium2 kernel written in Bass

```python
import math
import jax, jax.numpy as jnp
from jax import lax
import numpy as np

D_MODEL = 1024
BATCH = 4
SEQ = 4096
DEPTH = 4

CONV_WIDTH = D_MODEL // 2
CONV_KERNEL = 31
POOL_WIDTH = D_MODEL // 2
POOL_WINDOWS = (2, 4, 8, 16)
POOL_GROUP = POOL_WIDTH // len(POOL_WINDOWS)
ATT_HEADS = 8
ATT_HEAD_DIM = 128
ATT_WIDTH = ATT_HEADS * ATT_HEAD_DIM
MOBA_BLOCK = 256
MOBA_TOPK = 3
MOBA_QCHUNK = 64
N_BRANCH = 3
IN_SPLITS = (CONV_WIDTH, CONV_WIDTH, POOL_WIDTH, ATT_WIDTH, ATT_WIDTH, ATT_WIDTH)
IN_COLS = sum(IN_SPLITS) + N_BRANCH * D_MODEL
MEM_LEN = 256
XATTN_HEADS = 4
XATTN_HEAD_DIM = D_MODEL // XATTN_HEADS
FFN_HIDDEN = 2816
FFN_KERNEL = 3
LN_EPS = 1e-5
DEEPNORM_ALPHA = (2.0 * DEPTH) ** 0.25
DEEPNORM_BETA = (8.0 * DEPTH) ** -0.25

kernel_name = "hybrid_gated_conv_pool_moba_deepnorm"


def layer_norm(x, g, b):
    xf = x.astype(jnp.float32)
    mu = jnp.mean(xf, axis=-1, keepdims=True)
    var = jnp.mean(jnp.square(xf - mu), axis=-1, keepdims=True)
    y = (xf - mu) * lax.rsqrt(var + LN_EPS)
    return (y * g.astype(jnp.float32) + b.astype(jnp.float32)).astype(x.dtype)


def causal_depthwise_conv(x, w, b):
    k, c = w.shape
    y = lax.conv_general_dilated(
        x, w[:, None, :].astype(x.dtype), window_strides=(1,), padding=[(k - 1, 0)],
        dimension_numbers=("NWC", "WIO", "NWC"), feature_group_count=c)
    return y + b.astype(x.dtype)


def multiscale_pool(u):
    s = u.shape[1]
    t = jnp.arange(s)
    outs = []
    for gi, w in enumerate(POOL_WINDOWS):
        ug = u[..., gi * POOL_GROUP:(gi + 1) * POOL_GROUP].astype(jnp.float32)
        csum = jnp.cumsum(ug, axis=1)
        csum_lag = jnp.pad(csum, ((0, 0), (w, 0), (0, 0)))[:, :s]
        cnt = jnp.minimum(t + 1, w).astype(jnp.float32)[None, :, None]
        outs.append((csum - csum_lag) / cnt - ug)
    return jnp.concatenate(outs, axis=-1).astype(u.dtype)


def moba_attention(q, k, v):
    bsz, s, h, dh = q.shape
    nb = -(-s // MOBA_BLOCK)
    length = nb * MOBA_BLOCK
    pad = ((0, 0), (0, length - s), (0, 0), (0, 0))
    q, k, v = (jnp.pad(a, pad).transpose(0, 2, 1, 3) for a in (q, k, v))
    kb = k.reshape(bsz, h, nb, MOBA_BLOCK, dh)
    vb = v.reshape(bsz, h, nb, MOBA_BLOCK, dh)
    kmean = jnp.mean(kb.astype(jnp.float32), axis=3)
    gate = jnp.einsum("bhtd,bhnd->bhtn", q.astype(jnp.float32), kmean)
    qblk = jnp.arange(length) // MOBA_BLOCK
    past = jnp.arange(nb)[None, :] < qblk[:, None]
    gate = jnp.where(past[None, None], gate, -jnp.inf)
    kk = min(MOBA_TOPK, nb)
    _, sel = lax.top_k(gate, kk)
    sel_valid = jnp.arange(kk)[None, :] < jnp.minimum(qblk, MOBA_TOPK)[:, None]

    n_chunks = length // MOBA_QCHUNK
    q_c = jnp.moveaxis(q.reshape(bsz, h, n_chunks, MOBA_QCHUNK, dh), 2, 0)
    sel_c = jnp.moveaxis(sel.reshape(bsz, h, n_chunks, MOBA_QCHUNK, kk), 2, 0)
    valid_c = sel_valid.reshape(n_chunks, MOBA_QCHUNK, kk)
    scale = ATT_HEAD_DIM ** -0.5
    gather_blocks = jax.vmap(jax.vmap(lambda blocks, idx: blocks[idx]))

    def attend_chunk(args):
        qc, selc, validc, ci = args
        t0 = ci * MOBA_QCHUNK
        blk = t0 // MOBA_BLOCK
        own_k = lax.dynamic_index_in_dim(kb, blk, axis=2, keepdims=False)
        own_v = lax.dynamic_index_in_dim(vb, blk, axis=2, keepdims=False)
        qpos = t0 + jnp.arange(MOBA_QCHUNK)
        kpos = blk * MOBA_BLOCK + jnp.arange(MOBA_BLOCK)
        s_own = jnp.einsum("bhqd,bhkd->bhqk", qc, own_k).astype(jnp.float32) * scale
        s_own = jnp.where((kpos[None, :] <= qpos[:, None])[None, None], s_own, -jnp.inf)
        gk = gather_blocks(kb, selc)
        gv = gather_blocks(vb, selc)
        s_sel = jnp.einsum("bhqd,bhqnkd->bhqnk", qc, gk).astype(jnp.float32) * scale
        s_sel = jnp.where(validc[None, None, :, :, None], s_sel, -jnp.inf)
        scores = jnp.concatenate(
            [s_own, s_sel.reshape(bsz, h, MOBA_QCHUNK, kk * MOBA_BLOCK)], axis=-1)
        p = jax.nn.softmax(scores, axis=-1).astype(qc.dtype)
        p_own = p[..., :MOBA_BLOCK]
        p_sel = p[..., MOBA_BLOCK:].reshape(bsz, h, MOBA_QCHUNK, kk, MOBA_BLOCK)
        return (jnp.einsum("bhqk,bhkd->bhqd", p_own, own_v)
                + jnp.einsum("bhqnk,bhqnkd->bhqd", p_sel, gv))

    out = lax.map(attend_chunk, (q_c, sel_c, valid_c, jnp.arange(n_chunks)))
    out = jnp.moveaxis(out, 0, 2).reshape(bsz, h, length, dh)
    return out.transpose(0, 2, 1, 3)[:, :s]


def hybrid_mixer(x, w_in, b_in, conv_dw_w, conv_dw_b, conv_ln_g, conv_ln_b, conv_w_out,
                 pool_w, pool_scale, pool_w_out, att_w_out, mix_w_out):
    bsz, s, _ = x.shape
    z = x @ w_in + b_in
    cuts = [int(c) for c in np.cumsum(IN_SPLITS)]
    a_lin, a_gate, u_pool, q, k, v, gate_logits = jnp.split(z, cuts, axis=-1)
    a = a_lin * jax.nn.sigmoid(a_gate)
    a = causal_depthwise_conv(a, conv_dw_w, conv_dw_b)
    a = jax.nn.silu(layer_norm(a, conv_ln_g, conv_ln_b))
    y_a = a @ conv_w_out
    p = multiscale_pool(u_pool).reshape(bsz, s, len(POOL_WINDOWS), POOL_GROUP)
    p = jnp.einsum("bsgc,gcd->bsgd", p, pool_w).reshape(bsz, s, POOL_WIDTH) * pool_scale
    y_b = p @ pool_w_out
    shp = (bsz, s, ATT_HEADS, ATT_HEAD_DIM)
    o = moba_attention(q.reshape(shp), k.reshape(shp), v.reshape(shp)).reshape(bsz, s, ATT_WIDTH)
    y_c = o @ att_w_out
    g = jax.nn.sigmoid(gate_logits).reshape(bsz, s, N_BRANCH, D_MODEL)
    merged = g[:, :, 0] * y_a + g[:, :, 1] * y_b + g[:, :, 2] * y_c
    return merged @ mix_w_out


def memory_cross_attention(x, mem, wq, wkv, wo):
    bsz, s, _ = x.shape
    m = mem.shape[1]
    q = (x @ wq).reshape(bsz, s, XATTN_HEADS, XATTN_HEAD_DIM)
    kv = (mem @ wkv).reshape(bsz, m, 2, XATTN_HEADS, XATTN_HEAD_DIM)
    k, v = kv[:, :, 0], kv[:, :, 1]
    scores = jnp.einsum("bshd,bmhd->bhsm", q, k).astype(jnp.float32) * XATTN_HEAD_DIM ** -0.5
    p = jax.nn.softmax(scores, axis=-1).astype(v.dtype)
    o = jnp.einsum("bhsm,bmhd->bshd", p, v).reshape(bsz, s, D_MODEL)
    return o @ wo


def conv_ffn(x, w_up, dw_w, dw_b, w_down):
    h = causal_depthwise_conv(x @ w_up, dw_w, dw_b)
    a, b = jnp.split(h, 2, axis=-1)
    return (jax.nn.silu(a) * b) @ w_down


def setup_inputs(seed: int = 0) -> dict:
    key = jax.random.key(seed)
    ks = iter(jax.random.split(key, 32))
    f32 = jnp.float32

    def nrm(shape, scale):
        return jax.random.normal(next(ks), shape, f32) * scale

    def gain(shape):
        return 1.0 + 0.02 * jax.random.normal(next(ks), shape, f32)

    L = DEPTH
    return {
        "x": nrm((BATCH, SEQ, D_MODEL), 1.0),
        "mem": nrm((BATCH, MEM_LEN, D_MODEL), 1.0),
        "w_in": nrm((L, D_MODEL, IN_COLS), D_MODEL ** -0.5),
        "b_in": nrm((L, IN_COLS), 0.02),
        "conv_dw_w": nrm((L, CONV_KERNEL, CONV_WIDTH), CONV_KERNEL ** -0.5),
        "conv_dw_b": nrm((L, CONV_WIDTH), 0.02),
        "conv_ln_g": gain((L, CONV_WIDTH)),
        "conv_ln_b": nrm((L, CONV_WIDTH), 0.02),
        "conv_w_out": nrm((L, CONV_WIDTH, D_MODEL), CONV_WIDTH ** -0.5),
        "pool_w": nrm((L, len(POOL_WINDOWS), POOL_GROUP, POOL_GROUP), POOL_GROUP ** -0.5),
        "pool_scale": gain((L, POOL_WIDTH)),
        "pool_w_out": nrm((L, POOL_WIDTH, D_MODEL), POOL_WIDTH ** -0.5),
        "att_w_out": nrm((L, ATT_WIDTH, D_MODEL), ATT_WIDTH ** -0.5),
        "mix_w_out": nrm((L, D_MODEL, D_MODEL), DEEPNORM_BETA * D_MODEL ** -0.5),
        "ln1_g": gain((L, D_MODEL)),
        "ln1_b": nrm((L, D_MODEL), 0.02),
        "xa_wq": nrm((L, D_MODEL, D_MODEL), D_MODEL ** -0.5),
        "xa_wkv": nrm((L, D_MODEL, 2 * D_MODEL), D_MODEL ** -0.5),
        "xa_wo": nrm((L, D_MODEL, D_MODEL), DEEPNORM_BETA * D_MODEL ** -0.5),
        "ln2_g": gain((L, D_MODEL)),
        "ln2_b": nrm((L, D_MODEL), 0.02),
        "ffn_w_up": nrm((L, D_MODEL, 2 * FFN_HIDDEN), D_MODEL ** -0.5),
        "ffn_dw_w": nrm((L, FFN_KERNEL, 2 * FFN_HIDDEN), FFN_KERNEL ** -0.5),
        "ffn_dw_b": nrm((L, 2 * FFN_HIDDEN), 0.02),
        "ffn_w_down": nrm((L, FFN_HIDDEN, D_MODEL), DEEPNORM_BETA * FFN_HIDDEN ** -0.5),
        "ln3_g": gain((L, D_MODEL)),
        "ln3_b": nrm((L, D_MODEL), 0.02),
    }


def reference(x, mem, w_in, b_in, conv_dw_w, conv_dw_b, conv_ln_g, conv_ln_b, conv_w_out,
              pool_w, pool_scale, pool_w_out, att_w_out, mix_w_out, ln1_g, ln1_b,
              xa_wq, xa_wkv, xa_wo, ln2_g, ln2_b,
              ffn_w_up, ffn_dw_w, ffn_dw_b, ffn_w_down, ln3_g, ln3_b):
    for l in range(DEPTH):
        mix = hybrid_mixer(x, w_in[l], b_in[l], conv_dw_w[l], conv_dw_b[l], conv_ln_g[l],
                           conv_ln_b[l], conv_w_out[l], pool_w[l], pool_scale[l],
                           pool_w_out[l], att_w_out[l], mix_w_out[l])
        x = layer_norm(DEEPNORM_ALPHA * x + mix, ln1_g[l], ln1_b[l])
        xa = memory_cross_attention(x, mem, xa_wq[l], xa_wkv[l], xa_wo[l])
        x = layer_norm(DEEPNORM_ALPHA * x + xa, ln2_g[l], ln2_b[l])
        ff = conv_ffn(x, ffn_w_up[l], ffn_dw_w[l], ffn_dw_b[l], ffn_w_down[l])
        x = layer_norm(DEEPNORM_ALPHA * x + ff, ln3_g[l], ln3_b[l])
    return x
```

```python
import numpy as np
import ml_dtypes
from contextlib import ExitStack
import concourse.bass as bass
import concourse.mybir as mybir
from concourse.bass_utils import run_bass_kernel_spmd

F32 = mybir.dt.float32
BF16 = mybir.dt.bfloat16
AF = mybir.ActivationFunctionType
ALU = mybir.AluOpType
AX = mybir.AxisListType
NPBF = ml_dtypes.bfloat16

D = 1024
KC = 8
SEQ = 4096
BATCH = 4
DEPTH = 4
NT = 17
NTOK = NT * 128
OWN = 2048
H = 8
CK = 31
CHALO = 30
PHALO = 16
FH = 2816
FC = 22
XH = 4
MEM = 256
INA = 6656
ALPHA = (2.0 * DEPTH) ** 0.25
EPS = 1e-5
GROUPS = [(0, 128), (128, 640), (640, 1152), (1152, 1664), (1664, 2176)]
FFN_PASSES = [(0, 6), (6, 12), (12, 17)]
ENGS = ["pe", "act", "dve", "pool", "sp"]


def grp_of_tile(ti):
    return 0 if ti == 0 else 1 + (ti - 1) // 4


class Buf:
    __slots__ = ("name", "w", "r", "excl")

    def __init__(self, name, excl=False):
        self.name = name
        self.w = None
        self.r = []
        self.excl = excl


class Sched:
    NSLOT = 16

    def __init__(self, nc, es):
        self.nc = nc
        self.prog = {e: [] for e in ENGS}
        self.cnt = {e: 0 for e in ENGS}
        self.waited = {e: {} for e in ENGS}
        self.sem = {}
        for e in ENGS:
            self.sem[e] = es.enter_context(nc.semaphore("s_" + e))
        self.dq = {}
        for q in ("sp", "pool"):
            self.dq[q] = {"i": 0, "cnt": [0] * self.NSLOT}
            for s in range(self.NSLOT):
                self.sem[(q, s)] = es.enter_context(nc.semaphore(f"d_{q}{s}"))
        self.ninst = 0

    def buf(self, name="b", excl=False):
        return Buf(name, excl)

    def _deps(self, eng, reads, writes, is_dma=False):
        need = {}

        def add(ev):
            if ev is None:
                return
            k, v = ev
            if need.get(k, 0) < v:
                need[k] = v

        for b in reads:
            add(b.w)
            if b.excl:
                for ev in b.r:
                    if ev[0] != eng or is_dma:
                        add(ev)
        for b in writes:
            if b.w is not None and (is_dma or b.w[0] != eng):
                add(b.w)
            for ev in b.r:
                if is_dma or ev[0] != eng:
                    add(ev)
        if eng == "pe" and not is_dma:
            need.pop("pe", None)
        out = []
        wd = self.waited[eng]
        for k, v in need.items():
            if wd.get(k, 0) < v:
                wd[k] = v
                out.append((k, v))
        return out

    def _mark(self, ev, reads, writes):
        for b in reads:
            b.r.append(ev)
        for b in writes:
            b.w = ev
            b.r = []

    def op(self, eng, fn, reads=(), writes=()):
        waits = self._deps(eng, reads, writes)
        self.cnt[eng] += 1
        ev = (eng, self.cnt[eng])
        self.prog[eng].append((waits, fn, (eng, 1)))
        self._mark(ev, reads, writes)
        self.ninst += 1
        return ev

    def dma(self, q, out_ap, in_ap, reads=(), writes=()):
        st = self.dq[q]
        slot = st["i"] % self.NSLOT
        st["i"] += 1
        key = (q, slot)
        waits = self._deps(q, reads, writes, is_dma=True)
        prev = 16 * st["cnt"][slot]
        wd = self.waited[q]
        if prev > 0 and wd.get(key, 0) < prev:
            wd[key] = prev
            waits.append((key, prev))
        st["cnt"][slot] += 1
        ev = (key, 16 * st["cnt"][slot])
        self.prog[q].append((waits, lambda E: E.dma_start(out=out_ap, in_=in_ap), (key, 16)))
        self._mark(ev, reads, writes)
        self.ninst += 1
        return ev

    def barrier(self):
        allw = [(e, self.cnt[e]) for e in ENGS if self.cnt[e] > 0]
        for q in self.dq:
            st = self.dq[q]
            for s in range(self.NSLOT):
                if st["cnt"][s] > 0:
                    allw.append(((q, s), 16 * st["cnt"][s]))
        for e in ENGS:
            wd = self.waited[e]
            waits = []
            for k, v in allw:
                if k == e:
                    continue
                if wd.get(k, 0) < v:
                    wd[k] = v
                    waits.append((k, v))
            if waits:
                self.prog[e].append((waits, None, None))

    def mm(self, out, lhsT, rhs, start, stop, reads, writes):
        return self.op("pe", lambda E: E.matmul(out, lhsT=lhsT, rhs=rhs, start=start, stop=stop), reads, writes)

    def tr(self, out, in_, ident, reads, writes):
        return self.op("pe", lambda E: E.transpose(out=out, in_=in_, identity=ident), reads, writes)

    def act(self, out, in_, func, reads, writes, **kw):
        return self.op("act", lambda E: E.activation(out=out, in_=in_, func=func, **kw), reads, writes)

    def v(self, eng, name, reads, writes, **kw):
        return self.op(eng, lambda E: getattr(E, name)(**kw), reads, writes)

    def emit(self):
        nc = self.nc
        sem = self.sem
        prog = self.prog
        with nc.Block() as block:
            def run(E, lst):
                for waits, fn, inc in lst:
                    for k, v in waits:
                        E.wait_ge(sem[k], v)
                    if fn is not None:
                        fn(E).then_inc(sem[inc[0]], inc[1])

            @block.tensor
            def _(E):
                run(E, prog["pe"])

            @block.scalar
            def _(E):
                run(E, prog["act"])

            @block.vector
            def _(E):
                run(E, prog["dve"])

            @block.gpsimd
            def _(E):
                run(E, prog["pool"])

            @block.sync
            def _(E):
                run(E, prog["sp"])


def build(mode):
    nc = bass.Bass("TRN2", target_bir_lowering=False)
    fused = mode == "fused"
    if fused:
        phases = [("IN",)]
        for l in range(DEPTH):
            phases += [("A", l), ("X", l), ("B", l)]
        LA = LB = DEPTH
    elif mode == "first":
        phases = [("IN",), ("A", 0)]
        LA, LB = 1, 0
    elif mode == "mid":
        phases = [("B", 0), ("A", 0)]
        LA, LB = 1, 1
    else:
        phases = [("B", 0)]
        LA, LB = 0, 1
    hasA, hasB = LA > 0, LB > 0

    def din(name, shape, dt=F32):
        return nc.dram_tensor(name, list(shape), dt, kind="ExternalInput").ap()

    def dout(name, shape, dt=F32):
        return nc.dram_tensor(name, list(shape), dt, kind="ExternalOutput").ap()

    def dint(name, shape, dt=F32):
        return nc.dram_tensor(name, list(shape), dt, kind="Internal").ap()

    c_ident = din("ident", [128, 128])
    c_tri = din("tri", [128, 128])
    c_gbias = din("gate_bias", [128, NT * 16])
    c_gvalid = din("gate_valid", [128, NT * 16])
    c_halo = din("halo_flag", [128, 1])
    c_invcnt = din("invcnt", [128, 64])
    if mode in ("first", "fused"):
        x_in = din("x_in", [NT, 128, D])
    if hasA:
        w_in_a = din("w_in_a", [LA, D, INA])
        b_in_f = din("b_in_f", [LA, 128, 52])
        bv_d = din("b_v", [LA, 128, D])
        conv_w = din("conv_w", [LA, 128, 4 * 32])
        conv_ln = din("conv_ln", [LA, 128, 8])
        conv_w_out = din("conv_w_out", [LA, 512, D])
        pool_w = din("pool_w", [LA, 4, 128, 128])
        pool_scale = din("pool_scale", [LA, 128, 4])
        pool_w_out = din("pool_w_out", [LA, 512, D])
    if hasB:
        mem = din("mem", [MEM, D])
        w_g2 = din("w_g2", [LB, D, D])
        b_g2 = din("b_g2", [LB, 128, 8])
        att_w_out = din("att_w_out", [LB, D, D])
        mix_w_out = din("mix_w_out", [LB, D, D])
        ln_gb = din("ln_gb", [LB, 3, 128, 2 * D])
        xa_wq = din("xa_wq", [LB, D, D])
        xa_wkv = din("xa_wkv", [LB, D, 2 * D])
        xa_wo = din("xa_wo", [LB, D, D])
        ffn_w_up = din("ffn_w_up", [LB, D, 2 * FH])
        ffn_dw = din("ffn_dw", [LB, 128, 44 * 4])
        ffn_w_down = din("ffn_w_down", [LB, FH, D])
    if fused:
        xres = dint("xres", [NT, 128, D])
        qt_d = dint("qt", [H, 128, NTOK], BF16)
        kto_d = dint("kt_own", [H, 128, OWN], BF16)
        vo_d = dint("v_own", [16, 128, H * 129], BF16)
        ktp_d = dint("kt_past", [H, 128, OWN], BF16)
        vp_d = dint("v_past", [16, 128, H * 129], BF16)
        rd = dict(qt=qt_d, kto=kto_d, vo=vo_d, ktp=ktp_d, vp=vp_d, mg=None)
        wr = dict(qt=qt_d, kt=kto_d, v=vo_d, mg=None)
        y_out = dout("y", [16, 128, D])
    else:
        rd = wr = None
        if hasB:
            xres_i = din("xres_i", [NT, 128, D])
            rd = dict(qt=din("qt_i", [H, 128, NTOK], BF16), kto=din("kto_i", [H, 128, OWN], BF16),
                      vo=din("vo_i", [16, 128, H * 129], BF16), ktp=din("ktp_i", [H, 128, OWN], BF16),
                      vp=din("vp_i", [16, 128, H * 129], BF16), mg=din("mg_i", [128, KC * NTOK], BF16))
            xres = dint("xres", [NT, 128, D])
        if hasA:
            wr = dict(qt=dout("qt_o", [H, 128, NTOK], BF16), kt=dout("kt_o", [H, 128, OWN], BF16),
                      v=dout("v_o", [16, 128, H * 129], BF16), mg=dout("mg_o", [128, KC * NTOK], BF16))
        if mode == "mid":
            xres_o = dout("xres_o", [NT, 128, D])
        if mode == "last":
            y_out = dout("y", [16, 128, D])

    with ExitStack() as es:
        S = Sched(nc, es)

        tcount = {"i": 0}

        def T(stack, name, shape, dt):
            tcount["i"] += 1
            return stack.enter_context(nc.sbuf_tensor(f"{name}_{tcount['i']}", list(shape), dt))

        PS = [es.enter_context(nc.psum_tensor(f"ps{i}", [128, 512], F32)) for i in range(8)]
        PSB = [p.bitcast(BF16) for p in PS]
        PB = [S.buf(f"psb{i}", excl=True) for i in range(8)]
        rr = {"i": 0}

        def nb():
            b = rr["i"] % 6
            rr["i"] += 1
            return b

        trr = {"i": 0}

        def ntb():
            b = 6 + trr["i"] % 2
            trr["i"] += 1
            return b

        xT = T(es, "xT", [128, KC, NTOK], BF16)
        MGt = T(es, "MG", [128, KC * NTOK], BF16)
        MG = MGt[:, :].rearrange("p (c t) -> p c t", c=KC)
        identb = T(es, "identb", [128, 128], BF16)
        trib = T(es, "trib", [128, 128], BF16)
        ones_f = T(es, "ones_f", [128, 128], F32)
        ones_b = T(es, "ones_b", [128, 128], BF16)
        gbias = T(es, "gbias", [128, NT * 16], F32)
        gvalid = T(es, "gvalid", [128, NT * 16], F32)
        halo = T(es, "halo", [128, 1], F32)
        invcnt = T(es, "invcnt", [128, 64], F32)
        b_const = S.buf("const")
        b_xT = [S.buf(f"xT{g}") for g in range(5)]
        b_MG = [S.buf(f"MG{g}") for g in range(5)]
        S.dma("pool", identb[:], c_ident, writes=[b_const])
        S.dma("pool", trib[:], c_tri, writes=[b_const])
        S.dma("sp", gbias[:], c_gbias, writes=[b_const])
        S.dma("sp", gvalid[:], c_gvalid, writes=[b_const])
        S.dma("sp", halo[:], c_halo, writes=[b_const])
        S.dma("sp", invcnt[:], c_invcnt, writes=[b_const])
        S.v("dve", "memset", [], [b_const], ap=ones_f[:], constant=1.0)
        S.v("dve", "memset", [], [b_const], ap=ones_b[:], constant=1.0)
        if hasB:
            memT = T(es, "memT", [128, KC, MEM], BF16)
            b_memT = S.buf("memT")
        b_xres = [S.buf(f"xres{t}") for t in range(NT)]
        b_qt = [S.buf(f"qt{h}") for h in range(H)]
        b_kt = [S.buf(f"kt{h}") for h in range(H)]
        b_v = [S.buf(f"v{t}") for t in range(16)]
        b_past = S.buf("past")
        b_mgd = S.buf("mgd")

        def load_w(dst, src, buf, q="pool"):
            S.dma(q, dst, src, writes=[buf])

        def wview(w2d):
            return w2d.rearrange("(c p) n -> p c n", p=128)

        def to_xT(ti, xb_ap, xb_buf):
            tb = ntb()
            for c in range(KC):
                S.tr(PSB[tb][:, c * 128:(c + 1) * 128], xb_ap[:, c * 128:(c + 1) * 128], identb[:],
                     [xb_buf, b_const], [PB[tb]])
            S.v("dve", "tensor_copy", [PB[tb]], [b_xT[grp_of_tile(ti)]],
                out=xT[:, :, ti * 128:(ti + 1) * 128],
                in_=PSB[tb][:, :].rearrange("p (c n) -> p c n", c=KC))

        def load_xT_from(src):
            with ExitStack() as ps:
                xs = [T(ps, f"in_xs{i}", [128, D], F32) for i in range(2)]
                xb = [T(ps, f"in_xb{i}", [128, D], BF16) for i in range(2)]
                bxs = [S.buf() for _ in range(2)]
                bxb = [S.buf() for _ in range(2)]
                for ti in range(NT):
                    i = ti % 2
                    S.dma("sp", xs[i][:], src[ti], reads=[b_xres[ti]], writes=[bxs[i]])
                    S.act(xb[i][:], xs[i][:], AF.Copy, [bxs[i]], [bxb[i]])
                    to_xT(ti, xb[i], bxb[i])
                S.barrier()

        def phase_A(la):
            with ExitStack() as pa:
                bias_f = T(pa, "bias_f", [128, 52], F32)
                b_bias = S.buf()
                S.dma("sp", bias_f[:], b_in_f[la], writes=[b_bias])

                def proj_fm(wt, wcol0, ncols_w, wbuf, g0, g1):
                    b = nb()
                    n = g1 - g0
                    gi = [i for i, g in enumerate(GROUPS) if g[0] <= g0 < g[1]][0]
                    for kc in range(KC):
                        S.mm(PS[b][:, 0:n], wt[:, kc, wcol0:wcol0 + ncols_w], xT[:, kc, g0:g1], kc == 0, kc == KC - 1,
                             [wbuf, b_xT[gi]], [PB[b]])
                    return b

                with ExitStack() as p1:
                    wqk = [T(p1, f"wqk{i}", [128, KC, 256], BF16) for i in range(2)]
                    bwqk = [S.buf() for _ in range(2)]
                    qs = [T(p1, f"qs{i}", [128, NTOK], BF16) for i in range(2)]
                    ks = [T(p1, f"ks{i}", [128, OWN], BF16) for i in range(2)]
                    bqs = [S.buf() for _ in range(2)]
                    bks = [S.buf() for _ in range(2)]
                    wv = T(p1, "wv", [128, KC, D], BF16)
                    bwv = S.buf()
                    bv = T(p1, "bv", [128, D], F32)
                    bbv = S.buf()
                    vt = [T(p1, f"vt{i}", [128, H * 129], BF16) for i in range(2)]
                    bvt = [S.buf() for _ in range(2)]
                    for i in range(2):
                        S.v("dve", "memset", [], [bvt[i]], ap=vt[i][:], constant=1.0)
                    for hh in range(2):
                        load_w(wv[:, :, hh * 512:(hh + 1) * 512], wview(w_in_a[la, :, 3584 + hh * 512:3584 + (hh + 1) * 512]), bwv)
                    S.dma("sp", bv[:], bv_d[la], writes=[bbv])
                    for h in range(H):
                        i = h % 2
                        load_w(wqk[i][:, :, 0:128], wview(w_in_a[la, :, 1536 + h * 128:1536 + (h + 1) * 128]), bwqk[i])
                        load_w(wqk[i][:, :, 128:256], wview(w_in_a[la, :, 2560 + h * 128:2560 + (h + 1) * 128]), bwqk[i])
                        for (g0, g1) in GROUPS:
                            b = proj_fm(wqk[i], 0, 128, bwqk[i], g0, g1)
                            S.act(qs[i][:, g0:g1], PS[b][:, 0:g1 - g0], AF.Identity, [PB[b], b_bias], [bqs[i]],
                                  bias=bias_f[:, 12 + h:13 + h])
                        S.dma("sp", wr["qt"][h], qs[i][:], reads=[bqs[i]], writes=[b_qt[h]])
                        for (g0, g1) in GROUPS[1:]:
                            b = proj_fm(wqk[i], 128, 128, bwqk[i], g0, g1)
                            S.v("dve", "tensor_scalar", [PB[b], b_bias], [bks[i]],
                                out=ks[i][:, g0 - 128:g1 - 128], in0=PS[b][:, 0:g1 - g0],
                                scalar1=bias_f[:, 20 + h:21 + h], scalar2=None, op0=ALU.add)
                        S.dma("sp", wr["kt"][h], ks[i][:], reads=[bks[i]], writes=[b_kt[h]])
                    for t in range(1, NT):
                        i = t % 2
                        gi = grp_of_tile(t)
                        for hh in range(2):
                            b = nb()
                            for kc in range(KC):
                                S.mm(PS[b][:, :], xT[:, kc, t * 128:(t + 1) * 128], wv[:, kc, hh * 512:(hh + 1) * 512],
                                     kc == 0, kc == KC - 1, [bwv, b_xT[gi]], [PB[b]])
                            S.v("dve", "tensor_tensor", [PB[b], bbv], [bvt[i]],
                                out=vt[i][:, hh * 4 * 129:(hh + 1) * 4 * 129].rearrange("p (h d) -> p h d", h=4)[:, :, 0:128],
                                in0=PS[b][:, :].rearrange("p (h d) -> p h d", h=4),
                                in1=bv[:, hh * 512:(hh + 1) * 512].rearrange("p (h d) -> p h d", h=4), op=ALU.add)
                        S.dma("sp", wr["v"][t - 1], vt[i][:], reads=[bvt[i]], writes=[b_v[t - 1]])
                    S.barrier()

                with ExitStack() as p2:
                    acc = T(p2, "acc", [128, 4, NTOK], F32)
                    bacc = [S.buf() for _ in range(4)]
                    cw = T(p2, "cw", [128, 4 * 32], F32)
                    cln = T(p2, "cln", [128, 8], F32)
                    bcw = S.buf()
                    S.dma("sp", cw[:], conv_w[la], writes=[bcw])
                    S.dma("sp", cln[:], conv_ln[la], writes=[bcw])
                    with ExitStack() as p2a:
                        aG = T(p2a, "aG", [128, 4, CHALO + NTOK], F32)
                        baG = [S.buf() for _ in range(4)]
                        wl = [T(p2a, f"wl{i}", [128, KC, 256], BF16) for i in range(2)]
                        bwl = [S.buf() for _ in range(2)]
                        sig = [T(p2a, f"sig{i}", [128, 512], F32) for i in range(2)]
                        bsig = [S.buf() for _ in range(2)]
                        k = 0
                        for c in range(4):
                            i = c % 2
                            S.v("pool", "memset", [], [baG[c]], ap=aG[:, c, 0:CHALO], constant=0.0)
                            load_w(wl[i][:, :, 0:128], wview(w_in_a[la, :, c * 128:(c + 1) * 128]), bwl[i])
                            load_w(wl[i][:, :, 128:256], wview(w_in_a[la, :, 512 + c * 128:512 + (c + 1) * 128]), bwl[i])
                            for gi, (g0, g1) in enumerate(GROUPS):
                                n = g1 - g0
                                bl = proj_fm(wl[i], 0, 128, bwl[i], g0, g1)
                                bg = proj_fm(wl[i], 128, 128, bwl[i], g0, g1)
                                j = k % 2
                                k += 1
                                S.act(sig[j][:, 0:n], PS[bg][:, 0:n], AF.Sigmoid, [PB[bg], b_bias], [bsig[j]],
                                      bias=bias_f[:, 4 + c:5 + c])
                                S.v("dve", "scalar_tensor_tensor", [PB[bl], bsig[j], b_bias], [baG[c]],
                                    out=aG[:, c, CHALO + g0:CHALO + g1], in0=PS[bl][:, 0:n], scalar=bias_f[:, c:c + 1],
                                    in1=sig[j][:, 0:n], op0=ALU.add, op1=ALU.mult)
                                if gi == 0:
                                    S.v("pool", "tensor_scalar", [baG[c], b_const], [baG[c]],
                                        out=aG[:, c, CHALO:CHALO + 128], in0=aG[:, c, CHALO:CHALO + 128],
                                        scalar1=halo[:, 0:1], scalar2=None, op0=ALU.mult)
                            S.v("dve", "tensor_scalar", [baG[c], bcw], [bacc[c]],
                                out=acc[:, c, :], in0=aG[:, c, 0:NTOK], scalar1=cw[:, c * 32:c * 32 + 1],
                                scalar2=cw[:, c * 32 + 31:c * 32 + 32], op0=ALU.mult, op1=ALU.add)
                            for jj in range(1, CK):
                                S.v("dve", "scalar_tensor_tensor", [baG[c], bcw, bacc[c]], [bacc[c]],
                                    out=acc[:, c, :], in0=aG[:, c, jj:jj + NTOK], scalar=cw[:, c * 32 + jj:c * 32 + jj + 1],
                                    in1=acc[:, c, :], op0=ALU.mult, op1=ALU.add)
                        S.barrier()
                    with ExitStack() as p2b:
                        a_act = T(p2b, "a_act", [128, 4, NTOK], BF16)
                        ba_act = [S.buf() for _ in range(5)]
                        sq = [T(p2b, f"sq{i}", [128, 512], F32) for i in range(2)]
                        bsq = [S.buf() for _ in range(2)]
                        mean = T(p2b, "mean", [128, 512], F32)
                        msq = T(p2b, "msq", [128, 512], F32)
                        var = T(p2b, "var", [128, 512], F32)
                        rstd = T(p2b, "rstd", [128, 512], F32)
                        bst = S.buf()
                        tmp = [T(p2b, f"tmpc{i}", [128, 512], F32) for i in range(2)]
                        btmp = [S.buf() for _ in range(2)]
                        cwo = T(p2b, "cwo", [128, 4, D], BF16)
                        bcwo = S.buf()
                        load_w(cwo[:], wview(conv_w_out[la]), bcwo)
                        k = 0
                        for gi, (g0, g1) in enumerate(GROUPS):
                            n = g1 - g0
                            b1, b2 = nb(), nb()
                            for c in range(4):
                                S.mm(PS[b1][:, 0:n], ones_f[:], acc[:, c, g0:g1], c == 0, c == 3, [b_const, bacc[c]], [PB[b1]])
                            for c in range(4):
                                j = k % 2
                                k += 1
                                S.act(sq[j][:, 0:n], acc[:, c, g0:g1], AF.Square, [bacc[c]], [bsq[j]])
                                S.mm(PS[b2][:, 0:n], ones_f[:], sq[j][:, 0:n], c == 0, c == 3, [b_const, bsq[j]], [PB[b2]])
                            S.v("dve", "tensor_scalar", [PB[b1]], [bst], out=mean[:, 0:n], in0=PS[b1][:, 0:n],
                                scalar1=1.0 / 512, scalar2=None, op0=ALU.mult)
                            S.v("dve", "tensor_tensor", [bst], [bst], out=msq[:, 0:n], in0=mean[:, 0:n], in1=mean[:, 0:n], op=ALU.mult)
                            S.v("dve", "tensor_scalar", [bst], [bst], out=msq[:, 0:n], in0=msq[:, 0:n],
                                scalar1=-EPS, scalar2=None, op0=ALU.add)
                            S.v("dve", "scalar_tensor_tensor", [PB[b2], bst], [bst], out=var[:, 0:n], in0=PS[b2][:, 0:n],
                                scalar=1.0 / 512, in1=msq[:, 0:n], op0=ALU.mult, op1=ALU.subtract)
                            S.act(var[:, 0:n], var[:, 0:n], AF.Sqrt, [bst], [bst])
                            S.v("dve", "reciprocal", [bst], [bst], out=rstd[:, 0:n], in_=var[:, 0:n])
                            for c in range(4):
                                j = k % 2
                                k += 1
                                S.v("dve", "tensor_tensor", [bacc[c], bst], [btmp[j]], out=tmp[j][:, 0:n],
                                    in0=acc[:, c, g0:g1], in1=mean[:, 0:n], op=ALU.subtract)
                                S.v("pool", "tensor_tensor", [btmp[j], bst], [btmp[j]], out=tmp[j][:, 0:n],
                                    in0=tmp[j][:, 0:n], in1=rstd[:, 0:n], op=ALU.mult)
                                S.act(a_act[:, c, g0:g1], tmp[j][:, 0:n], AF.Silu, [btmp[j], bcw], [ba_act[gi]],
                                      scale=cln[:, 2 * c:2 * c + 1], bias=cln[:, 2 * c + 1:2 * c + 2])
                        wg = [T(p2b, f"wg0_{i}", [128, KC, 128], BF16) for i in range(2)]
                        bwg = [S.buf() for _ in range(2)]
                        sg = [T(p2b, f"sg0_{i}", [128, 512], BF16) for i in range(2)]
                        bsg = [S.buf() for _ in range(2)]
                        k = 0
                        for dm in range(KC):
                            i = dm % 2
                            load_w(wg[i][:], wview(w_in_a[la, :, 4608 + dm * 128:4608 + (dm + 1) * 128]), bwg[i])
                            for gi, (g0, g1) in enumerate(GROUPS):
                                n = g1 - g0
                                by = nb()
                                for c in range(4):
                                    S.mm(PS[by][:, 0:n], cwo[:, c, dm * 128:(dm + 1) * 128], a_act[:, c, g0:g1], c == 0, c == 3,
                                         [bcwo, ba_act[gi]], [PB[by]])
                                bg = proj_fm(wg[i], 0, 128, bwg[i], g0, g1)
                                j = k % 2
                                k += 1
                                S.act(sg[j][:, 0:n], PS[bg][:, 0:n], AF.Sigmoid, [PB[bg], b_bias], [bsg[j]],
                                      bias=bias_f[:, 36 + dm:37 + dm])
                                S.v("dve", "tensor_tensor", [PB[by], bsg[j]], [b_MG[gi]], out=MG[:, dm, g0:g1],
                                    in0=PS[by][:, 0:n], in1=sg[j][:, 0:n], op=ALU.mult)
                        S.barrier()

                with ExitStack() as p3:
                    uG = T(p3, "uG", [128, 4, PHALO + NTOK], F32)
                    buG = [S.buf() for _ in range(4)]
                    sA = T(p3, "sA", [128, PHALO + NTOK], F32)
                    sB = T(p3, "sB", [128, PHALO + NTOK], F32)
                    bsA, bsB = S.buf(), S.buf()
                    pp = T(p3, "pp", [128, 4, NTOK], BF16)
                    bpp = [S.buf() for _ in range(4)]
                    p2 = T(p3, "p2", [128, 4, NTOK], BF16)
                    bp2 = [S.buf() for _ in range(5)]
                    t16 = T(p3, "t16", [128, 16], F32)
                    bt16 = S.buf()
                    wu = [T(p3, f"wu{i}", [128, KC, 128], BF16) for i in range(2)]
                    bwu = [S.buf() for _ in range(2)]
                    pw = T(p3, "pw", [128, 4, 128], BF16)
                    psc = T(p3, "psc", [128, 4], F32)
                    pwo = T(p3, "pwo", [128, 4, D], BF16)
                    bpw = S.buf()
                    load_w(pw[:], pool_w[la].rearrange("g c d -> c g d"), bpw)
                    load_w(pwo[:], wview(pool_w_out[la]), bpw)
                    S.dma("sp", psc[:], pool_scale[la], writes=[bpw])
                    for g in range(4):
                        i = g % 2
                        w = 2 ** (g + 1)
                        S.v("pool", "memset", [], [buG[g]], ap=uG[:, g, 0:PHALO], constant=0.0)
                        load_w(wu[i][:], wview(w_in_a[la, :, 1024 + g * 128:1024 + (g + 1) * 128]), bwu[i])
                        for gi, (g0, g1) in enumerate(GROUPS):
                            n = g1 - g0
                            b = proj_fm(wu[i], 0, 128, bwu[i], g0, g1)
                            S.act(uG[:, g, PHALO + g0:PHALO + g1], PS[b][:, 0:n], AF.Identity, [PB[b], b_bias], [buG[g]],
                                  bias=bias_f[:, 8 + g:9 + g])
                            if gi == 0:
                                S.v("pool", "tensor_scalar", [buG[g], b_const], [buG[g]],
                                    out=uG[:, g, PHALO:PHALO + 128], in0=uG[:, g, PHALO:PHALO + 128],
                                    scalar1=halo[:, 0:1], scalar2=None, op0=ALU.mult)
                        L = PHALO + NTOK
                        src, bsrc = uG[:, g, :], buG[g]
                        dsts = [(sA, bsA), (sB, bsB)]
                        for kk in range(g + 1):
                            sh = 2 ** kk
                            dst, bdst = dsts[kk % 2]
                            S.v("dve", "tensor_tensor", [bsrc], [bdst], out=dst[:, sh:L], in0=src[:, sh:L], in1=src[:, 0:L - sh],
                                op=ALU.add)
                            src, bsrc = dst[:, :], bdst
                        S.v("dve", "scalar_tensor_tensor", [bsrc, buG[g]], [bpp[g]], out=pp[:, g, :], in0=src[:, PHALO:L],
                            scalar=1.0 / w, in1=uG[:, g, PHALO:L], op0=ALU.mult, op1=ALU.subtract)
                        S.v("dve", "tensor_tensor", [bsrc, b_const], [bt16], out=t16[:], in0=src[:, PHALO + 128:PHALO + 144],
                            in1=invcnt[:, g * 16:(g + 1) * 16], op=ALU.mult)
                        S.v("dve", "tensor_tensor", [bt16, buG[g]], [bpp[g]], out=pp[:, g, 128:144], in0=t16[:],
                            in1=uG[:, g, PHALO + 128:PHALO + 144], op=ALU.subtract)
                        for gi, (g0, g1) in enumerate(GROUPS):
                            n = g1 - g0
                            b = nb()
                            S.mm(PS[b][:, 0:n], pw[:, g, :], pp[:, g, g0:g1], True, True, [bpw, bpp[g]], [PB[b]])
                            S.act(p2[:, g, g0:g1], PS[b][:, 0:n], AF.Identity, [PB[b], bpw], [bp2[gi]], scale=psc[:, g:g + 1])
                    wg = [T(p3, f"wg1_{i}", [128, KC, 128], BF16) for i in range(2)]
                    bwg = [S.buf() for _ in range(2)]
                    sg = [T(p3, f"sg1_{i}", [128, 512], BF16) for i in range(2)]
                    bsg = [S.buf() for _ in range(2)]
                    tm = [T(p3, f"tm1_{i}", [128, 512], F32) for i in range(2)]
                    btm = [S.buf() for _ in range(2)]
                    k = 0
                    for dm in range(KC):
                        i = dm % 2
                        load_w(wg[i][:], wview(w_in_a[la, :, 5632 + dm * 128:5632 + (dm + 1) * 128]), bwg[i])
                        for gi, (g0, g1) in enumerate(GROUPS):
                            n = g1 - g0
                            by = nb()
                            for c in range(4):
                                S.mm(PS[by][:, 0:n], pwo[:, c, dm * 128:(dm + 1) * 128], p2[:, c, g0:g1], c == 0, c == 3,
                                     [bpw, bp2[gi]], [PB[by]])
                            bg = proj_fm(wg[i], 0, 128, bwg[i], g0, g1)
                            j = k % 2
                            k += 1
                            S.act(sg[j][:, 0:n], PS[bg][:, 0:n], AF.Sigmoid, [PB[bg], b_bias], [bsg[j]],
                                  bias=bias_f[:, 44 + dm:45 + dm])
                            S.v("dve", "tensor_tensor", [PB[by], bsg[j]], [btm[j]], out=tm[j][:, 0:n],
                                in0=PS[by][:, 0:n], in1=sg[j][:, 0:n], op=ALU.mult)
                            S.v("pool", "tensor_tensor", [btm[j], b_MG[gi]], [b_MG[gi]], out=MG[:, dm, g0:g1],
                                in0=MG[:, dm, g0:g1], in1=tm[j][:, 0:n], op=ALU.add)
                    if wr["mg"] is not None:
                        S.dma("sp", wr["mg"], MGt[:], reads=b_MG, writes=[b_mgd])
                    S.barrier()

        def ln_stage(pst, gb, bgb, tiles, emit_mm, src, dst, dst_tile0):
            xs = [T(pst, f"ln_xs{i}", [128, D], F32) for i in range(2)]
            s = [T(pst, f"ln_s{i}", [128, D], F32) for i in range(2)]
            xn = [T(pst, f"ln_xn{i}", [128, D], F32) for i in range(2)]
            xb = [T(pst, f"ln_xb{i}", [128, D], BF16) for i in range(2)]
            stt = [T(pst, f"ln_st{i}", [128, 12], F32) for i in range(2)]
            mv = [T(pst, f"ln_mv{i}", [128, 4], F32) for i in range(2)]
            bxs = [S.buf() for _ in range(2)]
            bs = [S.buf() for _ in range(2)]
            bxn = [S.buf() for _ in range(2)]
            bxb = [S.buf() for _ in range(2)]
            bmv = [S.buf() for _ in range(2)]

            def do_mm(ti):
                b0, b1 = nb(), nb()
                emit_mm(ti, b0, b1)
                return b0, b1

            cur = do_mm(tiles[0])
            for idx, ti in enumerate(tiles):
                nxt = do_mm(tiles[idx + 1]) if idx + 1 < len(tiles) else None
                i = idx % 2
                b0, b1 = cur
                S.dma("sp", xs[i][:], src[ti], reads=[b_xres[ti]], writes=[bxs[i]])
                for hh, b in enumerate((b0, b1)):
                    S.v("dve", "scalar_tensor_tensor", [bxs[i], PB[b]], [bs[i]], out=s[i][:, hh * 512:(hh + 1) * 512],
                        in0=xs[i][:, hh * 512:(hh + 1) * 512], scalar=ALPHA, in1=PS[b][:, :], op0=ALU.mult, op1=ALU.add)
                for hh in range(2):
                    S.v("dve", "bn_stats", [bs[i]], [bmv[i]], out=stt[i][:, hh * 6:(hh + 1) * 6], in_=s[i][:, hh * 512:(hh + 1) * 512])
                S.v("dve", "bn_aggr", [bmv[i]], [bmv[i]], out=mv[i][:, 0:2], in_=stt[i][:, :])
                S.v("dve", "tensor_scalar", [bmv[i]], [bmv[i]], out=mv[i][:, 2:3], in0=mv[i][:, 1:2], scalar1=EPS, scalar2=None,
                    op0=ALU.add)
                S.act(mv[i][:, 2:3], mv[i][:, 2:3], AF.Sqrt, [bmv[i]], [bmv[i]])
                S.v("dve", "reciprocal", [bmv[i]], [bmv[i]], out=mv[i][:, 3:4], in_=mv[i][:, 2:3])
                S.v("dve", "tensor_scalar", [bs[i], bmv[i]], [bxn[i]], out=xn[i][:], in0=s[i][:], scalar1=mv[i][:, 0:1],
                    scalar2=mv[i][:, 3:4], op0=ALU.subtract, op1=ALU.mult)
                S.v("pool", "tensor_tensor", [bxn[i], bgb], [bxn[i]], out=xn[i][:], in0=xn[i][:], in1=gb[:, 0:D], op=ALU.mult)
                S.v("pool", "tensor_tensor", [bxn[i], bgb], [bxn[i]], out=xn[i][:], in0=xn[i][:], in1=gb[:, D:2 * D], op=ALU.add)
                if ti >= dst_tile0:
                    S.dma("sp", dst[ti - dst_tile0], xn[i][:], reads=[bxn[i]], writes=[b_xres[ti]])
                S.act(xb[i][:], xn[i][:], AF.Copy, [bxn[i]], [bxb[i]])
                to_xT(ti, xb[i], bxb[i])
                cur = nxt

        def phase_B(lb, x_src, x_mid, x_dst, dst_tile0):
            SC = 128.0 ** -0.5
            with ExitStack() as pb:
                lngb = T(pb, "lngb", [128, 2 * D], F32)
                blngb = S.buf()
                with ExitStack() as p12:
                    OT = T(p12, "OT", [128, H, NTOK], BF16)
                    bOT = [S.buf() for _ in range(5)]
                    with ExitStack() as p1:
                        qh = [T(p1, f"qh{i}", [128, NTOK], BF16) for i in range(2)]
                        kh = [T(p1, f"kh{i}", [128, 2 * OWN], BF16) for i in range(2)]
                        bqh = [S.buf() for _ in range(2)]
                        bkh = [S.buf() for _ in range(2)]
                        vh = [T(p1, f"vh{i}", [128, 32, 258], BF16) for i in range(2)]
                        bvh = [S.buf() for _ in range(2)]
                        acc = T(p1, "att_acc", [128, NT, 129], F32)
                        bacc = [S.buf() for _ in range(NT)]
                        gm = T(p1, "gm", [128, NT * 16], F32)
                        sel = T(p1, "sel", [128, NT * 16], F32)
                        m8 = T(p1, "m8", [128, NT * 8], F32)
                        km = T(p1, "km", [128, 16], F32)
                        kmb = T(p1, "kmb", [128, 16], BF16)
                        bgm, bsel, bkm = S.buf(), S.buf(), S.buf()
                        pT = [[T(p1, f"pT{a}{b}", [128, 512], BF16) for b in range(2)] for a in range(2)]
                        bpT = [[S.buf() for _ in range(2)] for _ in range(2)]
                        rec = T(p1, "rec", [128, NT], F32)
                        brec = S.buf()
                        ob = T(p1, "ob", [128, NT, 128], BF16)
                        bob = [S.buf() for _ in range(NT)]
                        setc = {"i": 0}
                        for h in range(H):
                            i = h % 2
                            hp, hh = h // 2, h % 2
                            S.dma("sp", qh[i][:], rd["qt"][h], reads=[b_qt[h]], writes=[bqh[i]])
                            S.dma("sp", kh[i][:, 0:OWN], rd["ktp"][h], reads=[b_past], writes=[bkh[i]])
                            S.dma("sp", kh[i][:, OWN:2 * OWN], rd["kto"][h], reads=[b_kt[h]], writes=[bkh[i]])
                            if hh == 0:
                                vi = hp % 2
                                S.dma("sp", vh[vi][:, 0:16, :], rd["vp"].rearrange("k p c -> p k c")[:, :, hp * 258:(hp + 1) * 258],
                                      reads=[b_past], writes=[bvh[vi]])
                                S.dma("sp", vh[vi][:, 16:32, :], rd["vo"].rearrange("k p c -> p k c")[:, :, hp * 258:(hp + 1) * 258],
                                      reads=b_v, writes=[bvh[vi]])
                            vi = hp % 2
                            S.v("dve", "tensor_reduce", [bkh[i]], [bkm], out=km[:], in_=kh[i][:, :].rearrange("p (n k) -> p n k", k=256),
                                axis=AX.X, op=ALU.add)
                            S.v("dve", "tensor_scalar", [bkm], [bkm], out=kmb[:], in0=km[:], scalar1=1.0 / 256, scalar2=None,
                                op0=ALU.mult)
                            for ti in range(NT):
                                S.mm(PS[6][:, ti * 16:(ti + 1) * 16], qh[i][:, ti * 128:(ti + 1) * 128], kmb[:], True, True,
                                     [bqh[i], bkm], [PB[6]])
                            S.v("dve", "tensor_tensor", [PB[6], b_const], [bgm], out=gm[:], in0=PS[6][:, 0:NT * 16], in1=gbias[:],
                                op=ALU.add)
                            for ti in range(NT):
                                S.v("dve", "max", [bgm], [bsel], out=m8[:, ti * 8:(ti + 1) * 8], in_=gm[:, ti * 16:(ti + 1) * 16])
                            for ti in range(NT):
                                S.v("dve", "tensor_scalar", [bgm, bsel], [bsel], out=sel[:, ti * 16:(ti + 1) * 16],
                                    in0=gm[:, ti * 16:(ti + 1) * 16], scalar1=m8[:, ti * 8 + 2:ti * 8 + 3], scalar2=None,
                                    op0=ALU.is_ge)
                            S.v("dve", "tensor_tensor", [bsel, b_const], [bsel], out=sel[:], in0=sel[:], in1=gvalid[:], op=ALU.mult)
                            S.v("pool", "memset", [], bacc, ap=acc[:], constant=0.0)

                            def vslice(kt):
                                return vh[vi][:, kt, hh * 129:(hh + 1) * 129]

                            for j in range(16):
                                t_lo = max(15, 2 * j + 2)
                                tis = [T_ - 15 for T_ in range(t_lo, 32)]
                                for c0 in range(0, len(tis), 4):
                                    chunk = tis[c0:c0 + 4]
                                    n = len(chunk) * 128
                                    q0 = chunk[0] * 128
                                    a = setc["i"] % 2
                                    setc["i"] += 1
                                    for kk in range(2):
                                        kt = 2 * j + kk
                                        sb = 2 * a + kk
                                        S.mm(PS[sb][:, 0:n], kh[i][:, kt * 128:(kt + 1) * 128], qh[i][:, q0:q0 + n], True, True,
                                             [bkh[i], bqh[i]], [PB[sb]])
                                        S.act(pT[a][kk][:, 0:n], PS[sb][:, 0:n], AF.Exp, [PB[sb]], [bpT[a][kk]], scale=SC)
                                    for p0 in range(0, len(chunk), 2):
                                        pair = chunk[p0:p0 + 2]
                                        ob_ = 4 + (p0 // 2) % 2
                                        for s_, ti in enumerate(pair):
                                            ci = p0 + s_
                                            for kk in range(2):
                                                S.mm(PS[ob_][:, s_ * 129:(s_ + 1) * 129], pT[a][kk][:, ci * 128:(ci + 1) * 128],
                                                     vslice(2 * j + kk), kk == 0, kk == 1, [bpT[a][kk], bvh[vi]], [PB[ob_]])
                                        for s_, ti in enumerate(pair):
                                            S.v("dve", "scalar_tensor_tensor", [PB[ob_], bsel, bacc[ti]], [bacc[ti]],
                                                out=acc[:, ti, :], in0=PS[ob_][:, s_ * 129:(s_ + 1) * 129],
                                                scalar=sel[:, ti * 16 + j:ti * 16 + j + 1], in1=acc[:, ti, :],
                                                op0=ALU.mult, op1=ALU.add)
                                Tb = 2 * j + 1
                                if Tb < 15:
                                    continue
                                Ta = 2 * j
                                tia = Ta - 15 if Ta >= 15 else None
                                tib = Tb - 15
                                a = setc["i"] % 2
                                setc["i"] += 1
                                qlo = tia if tia is not None else tib
                                n = (tib - qlo + 1) * 128
                                S.mm(PS[2 * a][:, 0:n], kh[i][:, Ta * 128:(Ta + 1) * 128], qh[i][:, qlo * 128:qlo * 128 + n], True, True,
                                     [bkh[i], bqh[i]], [PB[2 * a]])
                                S.act(pT[a][0][:, 0:n], PS[2 * a][:, 0:n], AF.Exp, [PB[2 * a]], [bpT[a][0]], scale=SC)
                                if tia is not None:
                                    S.v("pool", "tensor_tensor", [bpT[a][0], b_const], [bpT[a][0]], out=pT[a][0][:, 0:128],
                                        in0=pT[a][0][:, 0:128], in1=trib[:], op=ALU.mult)
                                S.mm(PS[2 * a + 1][:, 0:128], kh[i][:, Tb * 128:(Tb + 1) * 128], qh[i][:, tib * 128:(tib + 1) * 128],
                                     True, True, [bkh[i], bqh[i]], [PB[2 * a + 1]])
                                S.act(pT[a][1][:, 0:128], PS[2 * a + 1][:, 0:128], AF.Exp, [PB[2 * a + 1]], [bpT[a][1]], scale=SC)
                                S.v("pool", "tensor_tensor", [bpT[a][1], b_const], [bpT[a][1]], out=pT[a][1][:, 0:128],
                                    in0=pT[a][1][:, 0:128], in1=trib[:], op=ALU.mult)
                                ob_ = 4
                                if tia is not None:
                                    S.mm(PS[ob_][:, 0:129], pT[a][0][:, 0:128], vslice(Ta), True, True, [bpT[a][0], bvh[vi]], [PB[ob_]])
                                cb = n - 128
                                S.mm(PS[ob_][:, 129:258], pT[a][0][:, cb:cb + 128], vslice(Ta), True, False, [bpT[a][0], bvh[vi]], [PB[ob_]])
                                S.mm(PS[ob_][:, 129:258], pT[a][1][:, 0:128], vslice(Tb), False, True, [bpT[a][1], bvh[vi]], [PB[ob_]])
                                if tia is not None:
                                    S.v("dve", "tensor_tensor", [PB[ob_], bacc[tia]], [bacc[tia]], out=acc[:, tia, :],
                                        in0=PS[ob_][:, 0:129], in1=acc[:, tia, :], op=ALU.add)
                                S.v("dve", "tensor_tensor", [PB[ob_], bacc[tib]], [bacc[tib]], out=acc[:, tib, :],
                                    in0=PS[ob_][:, 129:258], in1=acc[:, tib, :], op=ALU.add)
                            S.v("dve", "reciprocal", bacc, [brec], out=rec[:], in_=acc[:, :, 128])
                            for ti in range(NT):
                                S.v("dve", "tensor_scalar", [bacc[ti], brec], [bob[ti]], out=ob[:, ti, :], in0=acc[:, ti, 0:128],
                                    scalar1=rec[:, ti:ti + 1], scalar2=None, op0=ALU.mult)
                            for c0 in range(0, NT, 4):
                                chunk = list(range(c0, min(NT, c0 + 4)))
                                for s_, ti in enumerate(chunk):
                                    S.tr(PSB[7][:, s_ * 128:(s_ + 1) * 128], ob[:, ti, :], identb[:], [bob[ti], b_const], [PB[7]])
                                n = len(chunk) * 128
                                gset = sorted(set(grp_of_tile(ti) for ti in chunk))
                                S.act(OT[:, h, c0 * 128:c0 * 128 + n], PSB[7][:, 0:n], AF.Copy, [PB[7]], [bOT[g] for g in gset])
                        S.barrier()

                    with ExitStack() as p2:
                        wao = [T(p2, f"wao{i}", [128, H, 128], BF16) for i in range(2)]
                        wg = [T(p2, f"wg2_{i}", [128, KC, 128], BF16) for i in range(2)]
                        bwao = [S.buf() for _ in range(2)]
                        bwg = [S.buf() for _ in range(2)]
                        sg = [T(p2, f"sg2_{i}", [128, 512], BF16) for i in range(2)]
                        bsg = [S.buf() for _ in range(2)]
                        tm = [T(p2, f"tm2_{i}", [128, 512], F32) for i in range(2)]
                        btm = [S.buf() for _ in range(2)]
                        bg2 = T(p2, "bg2", [128, 8], F32)
                        bbg2 = S.buf()
                        mixw = T(p2, "mixw", [128, KC, D], BF16)
                        bmixw = S.buf()
                        S.dma("sp", bg2[:], b_g2[lb], writes=[bbg2])
                        S.dma("sp", lngb[:], ln_gb[lb, 0], writes=[blngb])
                        k = 0
                        for dm in range(KC):
                            i = dm % 2
                            load_w(wao[i][:], wview(att_w_out[lb, :, dm * 128:(dm + 1) * 128]), bwao[i])
                            load_w(wg[i][:], wview(w_g2[lb, :, dm * 128:(dm + 1) * 128]), bwg[i])
                            for gi, (g0, g1) in enumerate(GROUPS):
                                n = g1 - g0
                                by = nb()
                                for h in range(H):
                                    S.mm(PS[by][:, 0:n], wao[i][:, h, :], OT[:, h, g0:g1], h == 0, h == H - 1, [bwao[i], bOT[gi]], [PB[by]])
                                bg = nb()
                                for kc in range(KC):
                                    S.mm(PS[bg][:, 0:n], wg[i][:, kc, :], xT[:, kc, g0:g1], kc == 0, kc == KC - 1, [bwg[i], b_xT[gi]], [PB[bg]])
                                j = k % 2
                                k += 1
                                S.act(sg[j][:, 0:n], PS[bg][:, 0:n], AF.Sigmoid, [PB[bg], bbg2], [bsg[j]], bias=bg2[:, dm:dm + 1])
                                S.v("dve", "tensor_tensor", [PB[by], bsg[j]], [btm[j]], out=tm[j][:, 0:n], in0=PS[by][:, 0:n],
                                    in1=sg[j][:, 0:n], op=ALU.mult)
                                S.v("pool", "tensor_tensor", [btm[j], b_MG[gi]], [b_MG[gi]], out=MG[:, dm, g0:g1],
                                    in0=MG[:, dm, g0:g1], in1=tm[j][:, 0:n], op=ALU.add)
                        for hh in range(2):
                            load_w(mixw[:, :, hh * 512:(hh + 1) * 512], wview(mix_w_out[lb, :, hh * 512:(hh + 1) * 512]), bmixw)

                        def mm_mix(ti, b0, b1):
                            gi = grp_of_tile(ti)
                            for hh, b in enumerate((b0, b1)):
                                for dm in range(KC):
                                    S.mm(PS[b][:, :], MG[:, dm, ti * 128:(ti + 1) * 128], mixw[:, dm, hh * 512:(hh + 1) * 512],
                                         dm == 0, dm == KC - 1, [b_MG[gi], bmixw], [PB[b]])

                        ln_stage(p2, lngb, blngb, list(range(NT)), mm_mix, x_src, x_mid, 0)
                        S.barrier()

                with ExitStack() as p3:
                    OX = MG
                    kmT = T(p3, "kmT", [128, 8, MEM], BF16)
                    vm = T(p3, "vm", [128, 2, D], BF16)
                    bkv = S.buf()
                    wkv = [T(p3, f"wkv{i}", [128, KC, 256], BF16) for i in range(2)]
                    bwkv = [S.buf() for _ in range(2)]
                    wq = [T(p3, f"wq{i}", [128, KC, 256], BF16) for i in range(2)]
                    bwq = [S.buf() for _ in range(2)]
                    qx = [T(p3, f"qx{i}", [128, 2, NTOK], BF16) for i in range(2)]
                    bqx = [S.buf() for _ in range(2)]
                    pT = [[T(p3, f"xpT{a}{b}", [128, 512], BF16) for b in range(2)] for a in range(2)]
                    bpT = [[S.buf() for _ in range(2)] for _ in range(2)]
                    rec = [T(p3, f"xrec{i}", [128, 512], F32) for i in range(2)]
                    brec = [S.buf() for _ in range(2)]
                    wo = T(p3, "wo", [128, KC, D], BF16)
                    bwo = S.buf()
                    S.dma("sp", lngb[:], ln_gb[lb, 1], writes=[blngb])
                    XSC = 256.0 ** -0.5
                    for nt2 in range(8):
                        i = nt2 % 2
                        load_w(wkv[i][:], wview(xa_wkv[lb, :, nt2 * 256:(nt2 + 1) * 256]), bwkv[i])
                        if nt2 < 4:
                            for cc in range(2):
                                b = nb()
                                for kc in range(KC):
                                    S.mm(PS[b][:, 0:MEM], wkv[i][:, kc, cc * 128:(cc + 1) * 128], memT[:, kc, :], kc == 0, kc == KC - 1,
                                         [bwkv[i], b_memT], [PB[b]])
                                S.act(kmT[:, nt2 * 2 + cc, :], PS[b][:, 0:MEM], AF.Copy, [PB[b]], [bkv])
                        else:
                            c0 = (nt2 - 4) * 256
                            for mt in range(2):
                                b = nb()
                                for kc in range(KC):
                                    S.mm(PS[b][:, 0:256], memT[:, kc, mt * 128:(mt + 1) * 128], wkv[i][:, kc, :], kc == 0, kc == KC - 1,
                                         [bwkv[i], b_memT], [PB[b]])
                                S.act(vm[:, mt, c0:c0 + 256], PS[b][:, 0:256], AF.Copy, [PB[b]], [bkv])
                    k = 0
                    for hx in range(XH):
                        i = hx % 2
                        load_w(wq[i][:], wview(xa_wq[lb, :, hx * 256:(hx + 1) * 256]), bwq[i])
                        for gi, (g0, g1) in enumerate(GROUPS):
                            n = g1 - g0
                            for cc in range(2):
                                b = nb()
                                for kc in range(KC):
                                    S.mm(PS[b][:, 0:n], wq[i][:, kc, cc * 128:(cc + 1) * 128], xT[:, kc, g0:g1], kc == 0, kc == KC - 1,
                                         [bwq[i], b_xT[gi]], [PB[b]])
                                S.act(qx[i][:, cc, g0:g1], PS[b][:, 0:n], AF.Copy, [PB[b]], [bqx[i]])
                        for gi, (g0, g1) in enumerate(GROUPS):
                            n = g1 - g0
                            a = k % 2
                            k += 1
                            for mt in range(2):
                                b = nb()
                                for cc in range(2):
                                    S.mm(PS[b][:, 0:n], kmT[:, hx * 2 + cc, mt * 128:(mt + 1) * 128], qx[i][:, cc, g0:g1], cc == 0, cc == 1,
                                         [bkv, bqx[i]], [PB[b]])
                                S.act(pT[a][mt][:, 0:n], PS[b][:, 0:n], AF.Exp, [PB[b]], [bpT[a][mt]], scale=XSC)
                            br = nb()
                            for mt in range(2):
                                S.mm(PS[br][:, 0:n], ones_b[:], pT[a][mt][:, 0:n], mt == 0, mt == 1, [b_const, bpT[a][mt]], [PB[br]])
                            S.v("dve", "reciprocal", [PB[br]], [brec[a]], out=rec[a][:, 0:n], in_=PS[br][:, 0:n])
                            for cc in range(2):
                                b = nb()
                                for mt in range(2):
                                    S.mm(PS[b][:, 0:n], vm[:, mt, hx * 256 + cc * 128:hx * 256 + (cc + 1) * 128], pT[a][mt][:, 0:n],
                                         mt == 0, mt == 1, [bkv, bpT[a][mt]], [PB[b]])
                                S.v("dve", "tensor_tensor", [PB[b], brec[a]], [b_MG[gi]], out=OX[:, hx * 2 + cc, g0:g1],
                                    in0=PS[b][:, 0:n], in1=rec[a][:, 0:n], op=ALU.mult)
                    for hh in range(2):
                        load_w(wo[:, :, hh * 512:(hh + 1) * 512], wview(xa_wo[lb, :, hh * 512:(hh + 1) * 512]), bwo)

                    def mm_wo(ti, b0, b1):
                        gi = grp_of_tile(ti)
                        for hh, b in enumerate((b0, b1)):
                            for c in range(KC):
                                S.mm(PS[b][:, :], OX[:, c, ti * 128:(ti + 1) * 128], wo[:, c, hh * 512:(hh + 1) * 512],
                                     c == 0, c == KC - 1, [b_MG[gi], bwo], [PB[b]])

                    ln_stage(p3, lngb, blngb, list(range(NT)), mm_wo, x_mid, x_mid, 0)
                    S.barrier()

                with ExitStack() as p4:
                    wd = T(p4, "wd", [128, FC, D], BF16)
                    bwd = S.buf()
                    gT = MGt[:, 0:FC * 768].rearrange("p (c t) -> p c t", c=FC)
                    bgT = S.buf()
                    wup = [T(p4, f"wup{i}", [128, KC, 256], BF16) for i in range(2)]
                    bwup = [S.buf() for _ in range(2)]
                    hs = [[T(p4, f"hs{a}{b}", [128, 514], F32) for b in range(2)] for a in range(2)]
                    bhs = [[S.buf() for _ in range(2)] for _ in range(2)]
                    cc_ = [[T(p4, f"cc{a}{b}", [128, 512], F32) for b in range(2)] for a in range(2)]
                    bcc = [[S.buf() for _ in range(2)] for _ in range(2)]
                    sa = [T(p4, f"sa{a}", [128, 512], F32) for a in range(2)]
                    bsa = [S.buf() for _ in range(2)]
                    carry = T(p4, "carry", [128, 88], F32)
                    bcarry = [S.buf() for _ in range(44)]
                    fdw = T(p4, "fdw", [128, 176], F32)
                    bfdw = S.buf()
                    S.dma("sp", fdw[:], ffn_dw[lb], writes=[bfdw])
                    S.dma("sp", lngb[:], ln_gb[lb, 2], writes=[blngb])
                    S.v("pool", "memset", [], bcarry, ap=carry[:], constant=0.0)
                    for q4 in range(4):
                        load_w(wd[:, :, q4 * 256:(q4 + 1) * 256], wview(ffn_w_down[lb, :, q4 * 256:(q4 + 1) * 256]), bwd)
                    k = 0
                    for (t0, t1) in FFN_PASSES:
                        pg = []
                        tt = t0
                        while tt < t1:
                            te = min(t1, tt + 4)
                            pg.append((tt * 128, te * 128))
                            tt = te
                        for cp in range(FC):
                            i = cp % 2
                            load_w(wup[i][:, :, 0:128], wview(ffn_w_up[lb, :, cp * 128:(cp + 1) * 128]), bwup[i])
                            load_w(wup[i][:, :, 128:256], wview(ffn_w_up[lb, :, FH + cp * 128:FH + (cp + 1) * 128]), bwup[i])
                            for (g0, g1) in pg:
                                n = g1 - g0
                                a = k % 2
                                k += 1
                                gset = sorted(set(grp_of_tile(t) for t in range(g0 // 128, g1 // 128)))
                                for ab in range(2):
                                    ch = cp + ab * FC
                                    b = nb()
                                    for kc in range(KC):
                                        S.mm(PS[b][:, 0:n], wup[i][:, kc, ab * 128:(ab + 1) * 128], xT[:, kc, g0:g1], kc == 0, kc == KC - 1,
                                             [bwup[i]] + [b_xT[g] for g in gset], [PB[b]])
                                    hsb, bh = hs[a][ab], bhs[a][ab]
                                    S.v("pool", "tensor_copy", [bcarry[ch]], [bh], out=hsb[:, 0:2], in_=carry[:, 2 * ch:2 * ch + 2])
                                    S.act(hsb[:, 2:2 + n], PS[b][:, 0:n], AF.Identity, [PB[b]], [bh])
                                    if g0 == 0:
                                        S.v("pool", "tensor_scalar", [bh, b_const], [bh], out=hsb[:, 2:130], in0=hsb[:, 2:130],
                                            scalar1=halo[:, 0:1], scalar2=None, op0=ALU.mult)
                                    S.v("pool", "tensor_copy", [bh], [bcarry[ch]], out=carry[:, 2 * ch:2 * ch + 2], in_=hsb[:, n:n + 2])
                                    cb, bc = cc_[a][ab], bcc[a][ab]
                                    S.v("dve", "tensor_scalar", [bh, bfdw], [bc], out=cb[:, 0:n], in0=hsb[:, 2:2 + n],
                                        scalar1=fdw[:, ch * 4 + 2:ch * 4 + 3], scalar2=fdw[:, ch * 4 + 3:ch * 4 + 4], op0=ALU.mult, op1=ALU.add)
                                    S.v("dve", "scalar_tensor_tensor", [bh, bfdw, bc], [bc], out=cb[:, 0:n], in0=hsb[:, 1:1 + n],
                                        scalar=fdw[:, ch * 4 + 1:ch * 4 + 2], in1=cb[:, 0:n], op0=ALU.mult, op1=ALU.add)
                                    S.v("dve", "scalar_tensor_tensor", [bh, bfdw, bc], [bc], out=cb[:, 0:n], in0=hsb[:, 0:n],
                                        scalar=fdw[:, ch * 4:ch * 4 + 1], in1=cb[:, 0:n], op0=ALU.mult, op1=ALU.add)
                                S.act(sa[a][:, 0:n], cc_[a][0][:, 0:n], AF.Silu, [bcc[a][0]], [bsa[a]])
                                S.v("pool", "tensor_tensor", [bsa[a], bcc[a][1]], [bgT], out=gT[:, cp, g0 - t0 * 128:g1 - t0 * 128],
                                    in0=sa[a][:, 0:n], in1=cc_[a][1][:, 0:n], op=ALU.mult)

                        def mm_down(ti, b0, b1, t0=t0):
                            for hh, b in enumerate((b0, b1)):
                                for c in range(FC):
                                    S.mm(PS[b][:, :], gT[:, c, (ti - t0) * 128:(ti - t0 + 1) * 128], wd[:, c, hh * 512:(hh + 1) * 512],
                                         c == 0, c == FC - 1, [bgT, bwd], [PB[b]])

                        with ExitStack() as pln:
                            ln_stage(pln, lngb, blngb, list(range(t0, t1)), mm_down, x_mid, x_dst, dst_tile0)
                            S.barrier()
                S.barrier()

        if hasB:
            with ExitStack() as pm:
                ms = T(pm, "ms", [128, 2, D], BF16)
                bms = S.buf()
                S.dma("pool", ms[:], mem.rearrange("(t p) d -> p t d", p=128), writes=[bms])
                for mt in range(2):
                    tb = ntb()
                    for c in range(KC):
                        S.tr(PSB[tb][:, c * 128:(c + 1) * 128], ms[:, mt, c * 128:(c + 1) * 128], identb[:], [bms, b_const], [PB[tb]])
                    S.v("dve", "tensor_copy", [PB[tb]], [b_memT], out=memT[:, :, mt * 128:(mt + 1) * 128],
                        in_=PSB[tb][:, :].rearrange("p (c n) -> p c n", c=KC))
                S.barrier()

        first_B = True
        for ph in phases:
            if ph[0] == "IN":
                load_xT_from(x_in)
            elif ph[0] == "A":
                phase_A(ph[1])
            elif ph[0] == "X":
                raise NotImplementedError
            elif ph[0] == "B":
                lb = ph[1]
                if not fused:
                    load_xT_from(xres_i)
                    S.dma("sp", MGt[:], rd["mg"], writes=b_MG)
                    x_src = xres_i
                    if mode == "mid":
                        x_dst, d0 = xres_o, 0
                    else:
                        x_dst, d0 = y_out, 1
                else:
                    x_src = x_in if first_B else xres
                    if lb == DEPTH - 1:
                        x_dst, d0 = y_out, 1
                    else:
                        x_dst, d0 = xres, 0
                first_B = False
                phase_B(lb, x_src, xres, x_dst, d0)
        S.barrier()
        S.emit()
    return nc, S.ninst


_CACHE = {}


def _get_nc(mode):
    if mode not in _CACHE:
        _CACHE[mode] = build(mode)[0]
    return _CACHE[mode]


def _consts(core):
    half = core % 2
    ident = np.eye(128, dtype=np.float32)
    kk = np.arange(128)[:, None]
    qq = np.arange(128)[None, :]
    tri = (qq >= kk).astype(np.float32)
    gb = np.zeros((NT, 16), np.float32)
    gv = np.zeros((NT, 16), np.float32)
    for ti in range(NT):
        jo = (15 + ti) // 2
        for j in range(16):
            ok = (j < jo) and (half == 1 or j >= 8)
            gv[ti, j] = 1.0 if ok else 0.0
            gb[ti, j] = 0.0 if ok else -1e30
    gb = np.broadcast_to(gb.reshape(1, -1), (128, NT * 16)).copy()
    gv = np.broadcast_to(gv.reshape(1, -1), (128, NT * 16)).copy()
    halo = np.full((128, 1), float(half), np.float32)
    inv = np.zeros((4, 16), np.float32)
    for g, w in enumerate((2, 4, 8, 16)):
        for t in range(16):
            inv[g, t] = 1.0 / w if half == 1 else 1.0 / min(t + 1, w)
    inv = np.broadcast_to(inv.reshape(1, -1), (128, 64)).copy()
    return dict(ident=ident, tri=tri, gate_bias=gb, gate_valid=gv, halo_flag=halo, invcnt=inv)


def _fm(vec, ntile):
    return np.ascontiguousarray(vec.reshape(ntile, 128).T)


def _a_weights(inp, ls):
    out = {}
    out["w_in_a"] = np.ascontiguousarray(np.stack([inp["w_in"][l][:, :INA] for l in ls]))
    out["b_in_f"] = np.stack([_fm(inp["b_in"][l][:INA], 52) for l in ls])
    out["b_v"] = np.stack([np.broadcast_to(inp["b_in"][l][3584:4608][None, :], (128, D)) for l in ls]).copy()
    cw = []
    for l in ls:
        w = inp["conv_dw_w"][l]
        a = np.zeros((128, 4, 32), np.float32)
        a[:, :, :31] = w.T.reshape(4, 128, 31).transpose(1, 0, 2)
        a[:, :, 31] = inp["conv_dw_b"][l].reshape(4, 128).T
        cw.append(a.reshape(128, 128))
    out["conv_w"] = np.stack(cw)
    cl = []
    for l in ls:
        a = np.zeros((128, 4, 2), np.float32)
        a[:, :, 0] = inp["conv_ln_g"][l].reshape(4, 128).T
        a[:, :, 1] = inp["conv_ln_b"][l].reshape(4, 128).T
        cl.append(a.reshape(128, 8))
    out["conv_ln"] = np.stack(cl)
    out["conv_w_out"] = np.ascontiguousarray(np.stack([inp["conv_w_out"][l] for l in ls]))
    out["pool_w"] = np.ascontiguousarray(np.stack([inp["pool_w"][l] for l in ls]))
    out["pool_scale"] = np.stack([_fm(inp["pool_scale"][l], 4) for l in ls])
    out["pool_w_out"] = np.ascontiguousarray(np.stack([inp["pool_w_out"][l] for l in ls]))
    return out


def _b_weights(inp, ls):
    out = {}
    out["w_g2"] = np.ascontiguousarray(np.stack([inp["w_in"][l][:, INA:] for l in ls]))
    out["b_g2"] = np.stack([_fm(inp["b_in"][l][INA:], 8) for l in ls])
    for k in ("att_w_out", "mix_w_out", "xa_wq", "xa_wkv", "xa_wo", "ffn_w_up", "ffn_w_down"):
        out[k] = np.ascontiguousarray(np.stack([inp[k][l] for l in ls]))
    gbs = []
    for l in ls:
        a = np.zeros((3, 128, 2 * D), np.float32)
        for i, (g, b) in enumerate((("ln1_g", "ln1_b"), ("ln2_g", "ln2_b"), ("ln3_g", "ln3_b"))):
            a[i, :, :D] = inp[g][l][None, :]
            a[i, :, D:] = inp[b][l][None, :]
        gbs.append(a)
    out["ln_gb"] = np.stack(gbs)
    fd = []
    for l in ls:
        a = np.zeros((128, 44, 4), np.float32)
        a[:, :, 0:3] = inp["ffn_dw_w"][l].T.reshape(44, 128, 3).transpose(1, 0, 2)
        a[:, :, 3] = inp["ffn_dw_b"][l].reshape(44, 128).T
        fd.append(a.reshape(128, 176))
    out["ffn_dw"] = np.stack(fd)
    return out


def _x_tiles(x, core):
    b, half = core // 2, core % 2
    xt = np.zeros((NT, 128, D), np.float32)
    if half == 0:
        xt[1:] = x[b, 0:OWN].reshape(16, 128, D)
    else:
        xt[:] = x[b, OWN - 128:SEQ].reshape(NT, 128, D)
    return xt


def kernel(**inputs):
    inp = {k: np.asarray(v) for k, v in inputs.items()}
    n = 8
    cores = list(range(n))
    consts = [_consts(c) for c in cores]
    xts = [_x_tiles(inp["x"], c) for c in cores]
    mems = [np.ascontiguousarray(inp["mem"][c // 2]) for c in cores]

    aw = _a_weights(inp, [0])
    maps = [dict(consts[c], x_in=xts[c], **aw) for c in cores]
    res = run_bass_kernel_spmd(_get_nc("first"), maps, core_ids=cores).results
    xres = xts
    out = None
    for l in range(DEPTH):
        hand = []
        for c in cores:
            r = res[c]
            p = res[c - (c % 2)]
            hand.append(dict(qt_i=r["qt_o"], kto_i=r["kt_o"], vo_i=r["v_o"], ktp_i=p["kt_o"], vp_i=p["v_o"], mg_i=r["mg_o"]))
        bw = _b_weights(inp, [l])
        if l < DEPTH - 1:
            aw = _a_weights(inp, [l + 1])
            maps = [dict(consts[c], xres_i=xres[c], mem=mems[c], **hand[c], **bw, **aw) for c in cores]
            res = run_bass_kernel_spmd(_get_nc("mid"), maps, core_ids=cores).results
            xres = [res[c]["xres_o"] for c in cores]
        else:
            maps = [dict(consts[c], xres_i=xres[c], mem=mems[c], **hand[c], **bw) for c in cores]
            res = run_bass_kernel_spmd(_get_nc("last"), maps, core_ids=cores).results
            out = np.zeros((BATCH, SEQ, D), np.float32)
            for c in cores:
                b, half = c // 2, c % 2
                out[b, half * OWN:(half + 1) * OWN] = np.asarray(res[c]["y"], dtype=np.float32).reshape(OWN, D)
    return out
```
